# Optimizing a Trainium2 kernel written in Bass

```python
import jax, jax.numpy as jnp
from jax import lax
import numpy as np


D_MODEL = 1024
BATCH = 8
SEQ = 4096
DEPTH = 4

ATT_HEAD_DIM = 64
D_ATT = D_MODEL // 2
N_ATT_HEADS = D_ATT // ATT_HEAD_DIM
D_CONV = D_MODEL // 2
N_CONV_GROUPS = 8
CONV_WIDTH = 31
Q_BLOCK = 128
LN_EPS = 1e-5
GATE_INIT_STD = 0.02
DEEPNORM_ALPHA = (2 * DEPTH) ** 0.25
DEEPNORM_BETA = (8 * DEPTH) ** -0.25
IN_WIDTHS = (D_ATT, D_ATT, D_ATT, D_ATT, D_CONV, D_CONV, D_CONV, D_MODEL, D_MODEL)
D_IN = sum(IN_WIDTHS)

kernel_name = 'stickbreak_conformer_gated_hybrid'


def _split_points():
    pts, acc = [], 0
    for w in IN_WIDTHS[:-1]:
        acc += w
        pts.append(acc)
    return pts


def layer_norm(x, g, b):
    xf = x.astype(jnp.float32)
    mu = jnp.mean(xf, axis=-1, keepdims=True)
    var = jnp.mean(jnp.square(xf - mu), axis=-1, keepdims=True)
    return ((xf - mu) * lax.rsqrt(var + LN_EPS) * g + b).astype(x.dtype)


def stick_breaking_attention(q, k, v):
    S = q.shape[1]
    scale = ATT_HEAD_DIM ** -0.5
    outs = []
    for blk in range(S // Q_BLOCK):
        q0 = blk * Q_BLOCK
        q1 = q0 + Q_BLOCK
        qb = q[:, q0:q1]
        kb = k[:, :q1]
        vb = v[:, :q1]
        z = jnp.einsum('bthd,bshd->bhts', qb, kb).astype(jnp.float32) * scale
        t_pos = q0 + jnp.arange(Q_BLOCK)[:, None]
        s_pos = jnp.arange(q1)[None, :]
        causal = s_pos < t_pos
        log_fail = jnp.where(causal, jax.nn.log_sigmoid(-z), 0.0)
        later = lax.cumsum(log_fail, axis=3, reverse=True) - log_fail
        w = jnp.where(causal, jnp.exp(jax.nn.log_sigmoid(z) + later), 0.0)
        outs.append(jnp.einsum('bhts,bshd->bthd', w.astype(vb.dtype), vb))
    return jnp.concatenate(outs, axis=1)


def causal_depthwise_conv(u, w, b):
    C = u.shape[-1]
    up = jnp.pad(u, ((0, 0), (CONV_WIDTH - 1, 0), (0, 0)))
    out = lax.conv_general_dilated(up, w[:, None, :].astype(u.dtype), window_strides=(1,), padding='VALID',
                                   dimension_numbers=('NWC', 'WIO', 'NWC'), feature_group_count=C)
    return out + b


def hybrid_layer(x, w_in, b_in, conv_w, conv_b, conv_ln_g, conv_ln_b,
                 w_att_proj, w_conv_proj, b_conv_proj, w_out, ln_g, ln_b):
    B, S, _ = x.shape
    u = jnp.einsum('bsd,de->bse', x, w_in) + b_in
    q, k, v, z_att, glu_a, glu_b, z_conv, g_att, g_conv = jnp.split(u, _split_points(), axis=-1)

    heads = lambda t: t.reshape(B, S, N_ATT_HEADS, ATT_HEAD_DIM)
    att = stick_breaking_attention(heads(q), heads(k), heads(v)).reshape(B, S, D_ATT)
    att_branch = jnp.einsum('bsc,cd->bsd', att * jax.nn.silu(z_att), w_att_proj)

    c = glu_a * jax.nn.sigmoid(glu_b)
    c = causal_depthwise_conv(c, conv_w, conv_b)
    c = jax.nn.silu(layer_norm(c, conv_ln_g, conv_ln_b))
    conv_branch = jnp.einsum('bsc,cd->bsd', c * jax.nn.silu(z_conv), w_conv_proj) + b_conv_proj

    merged = jax.nn.sigmoid(g_att) * att_branch + jax.nn.sigmoid(g_conv) * conv_branch
    y = jnp.einsum('bsd,de->bse', merged, w_out)
    return layer_norm(DEEPNORM_ALPHA * x + y, ln_g, ln_b)


def setup_inputs(seed: int = 0) -> dict:
    key = jax.random.key(seed)
    ks = jax.random.split(key, 13)
    L, D = DEPTH, D_MODEL
    nrm = lambda k, shape, s: jax.random.normal(k, shape, jnp.float32) * s
    x = nrm(ks[0], (BATCH, SEQ, D), 1.0)
    col_scale = jnp.ones((D_IN,), jnp.float32).at[2 * D_ATT:3 * D_ATT].set(DEEPNORM_BETA)
    w_in = nrm(ks[1], (L, D, D_IN), D ** -0.5) * col_scale
    b_in = nrm(ks[2], (L, D_IN), GATE_INIT_STD)
    conv_w = nrm(ks[3], (L, CONV_WIDTH, D_CONV), CONV_WIDTH ** -0.5)
    conv_b = nrm(ks[4], (L, D_CONV), GATE_INIT_STD)
    conv_ln_g = 1.0 + nrm(ks[5], (L, D_CONV), GATE_INIT_STD)
    conv_ln_b = nrm(ks[6], (L, D_CONV), GATE_INIT_STD)
    w_att_proj = nrm(ks[7], (L, D_ATT, D), D_ATT ** -0.5 * DEEPNORM_BETA)
    w_conv_proj = nrm(ks[8], (L, D_CONV, D), D_CONV ** -0.5 * DEEPNORM_BETA)
    b_conv_proj = nrm(ks[9], (L, D), GATE_INIT_STD)
    w_out = nrm(ks[10], (L, D, D), D ** -0.5 * DEEPNORM_BETA)
    ln_g = 1.0 + nrm(ks[11], (L, D), GATE_INIT_STD)
    ln_b = nrm(ks[12], (L, D), GATE_INIT_STD)
    return {'x': x, 'w_in': w_in, 'b_in': b_in, 'conv_w': conv_w, 'conv_b': conv_b,
            'conv_ln_g': conv_ln_g, 'conv_ln_b': conv_ln_b, 'w_att_proj': w_att_proj,
            'w_conv_proj': w_conv_proj, 'b_conv_proj': b_conv_proj, 'w_out': w_out,
            'ln_g': ln_g, 'ln_b': ln_b}


def reference(x, w_in, b_in, conv_w, conv_b, conv_ln_g, conv_ln_b,
              w_att_proj, w_conv_proj, b_conv_proj, w_out, ln_g, ln_b):
    for l in range(DEPTH):
        x = hybrid_layer(x, w_in[l], b_in[l], conv_w[l], conv_b[l], conv_ln_g[l], conv_ln_b[l],
                         w_att_proj[l], w_conv_proj[l], b_conv_proj[l], w_out[l], ln_g[l], ln_b[l])
    return x
```

```python
from contextlib import ExitStack

import numpy as np
import concourse.bass as bass
import concourse.mybir as mybir
from concourse.bass_utils import run_bass_kernel_spmd

F32 = mybir.dt.float32
BF16 = mybir.dt.bfloat16
AF = mybir.ActivationFunctionType
ALU = mybir.AluOpType

D = 1024
DEPTH = 4
SEQ = 4096
NP = 204
ALPHA = float((2 * DEPTH) ** 0.25)
LN_EPS = 1e-5
NDUM = 2
MODE = "fused"


class Src:
    def __init__(self, name):
        self.name = name
        self.sem = None
        self.count = 0


class Eng(Src):
    def __init__(self, name):
        super().__init__(name)
        self.ops = []
        self.seen = {}


class V:
    def __init__(self, t, ap):
        self.t = t
        self.ap = ap


class Tl(V):
    def __init__(self, ap, name=""):
        self.t = self
        self.ap = ap
        self.w = None
        self.r = {}
        self.name = name
        self.dsem = None
        self.rng = None

    def __getitem__(self, k):
        return V(self, self.ap[k])


def build(S, L, dbg=None):
    NQ = S // 512
    nc = bass.Bass("TRN2", target_bir_lowering=False)
    xT = nc.dram_tensor("xT", [D, S], F32, kind="ExternalInput").ap()
    wA = nc.dram_tensor("wA", [L, 4, 128, 4096], F32, kind="ExternalInput").ap()
    wB = nc.dram_tensor("wB", [L, 28, 128, 1024], F32, kind="ExternalInput").ap()
    wAP = nc.dram_tensor("wAP", [L, 8, 128, 512], F32, kind="ExternalInput").ap()
    wCP = nc.dram_tensor("wCP", [L, 8, 128, 512], F32, kind="ExternalInput").ap()
    wO = nc.dram_tensor("wO", [L, 8, 128, 1024], F32, kind="ExternalInput").ap()
    pcol_d = nc.dram_tensor("pcol", [128, L * NP], F32, kind="ExternalInput").ap()
    bv_d = nc.dram_tensor("bv", [L, 512], F32, kind="ExternalInput").ap()
    cst_d = nc.dram_tensor("cst", [128, 384], F32, kind="ExternalInput").ap()
    outT = nc.dram_tensor("outT", [D, S], F32, kind="ExternalOutput").ap()

    PE, ACT, DVE, POOL, SP = Eng("pe"), Eng("act"), Eng("dve"), Eng("pool"), Eng("sp")
    engines = [PE, ACT, DVE, POOL, SP]
    dsems = []

    es = ExitStack()
    with es:
        def sb(name, shape, dt):
            return es.enter_context(nc.sbuf_tensor(name, shape, dt))

        for e in engines:
            e.sem = es.enter_context(nc.semaphore("s_" + e.name))

        def getsem(t):
            if t.dsem is None:
                s = Src("d%d" % len(dsems))
                s.sem = es.enter_context(nc.semaphore(s.name))
                dsems.append(s)
                t.dsem = s
            return t.dsem

        xhi = sb("xhi", [128, 8 * S], BF16)
        xlo = sb("xlo", [128, 8 * S], BF16)
        gatt = sb("gatt", [128, 4 * S], BF16)
        tri_s = sb("tri", [128, 128], BF16)
        onesb_s = sb("onesb", [128, 128], BF16)
        mask_s = sb("mask", [128, 128], BF16)
        ident_s = sb("ident", [128, 128], BF16)
        onesD_s = sb("onesD", [128, 128], F32)
        onesC_s = sb("onesC", [128, 128], F32)
        pc_s = sb("pc", [128, NP], F32)
        nbz_s = sb("nbz", [128, 4], F32)
        ARENA = 45568
        arena = sb("arena", [128, ARENA // 2], BF16)
        psum = [es.enter_context(nc.psum_tensor("ps%d" % i, [128, 512], F32)) for i in range(8)]

        xhi_t = [[Tl(xhi[:, c * S + q * 512: c * S + (q + 1) * 512]) for q in range(NQ)] for c in range(8)]
        xlo_t = [[Tl(xlo[:, c * S + q * 512: c * S + (q + 1) * 512]) for q in range(NQ)] for c in range(8)]
        gatt_t = [[Tl(gatt[:, c * S + q * 512: c * S + (q + 1) * 512]) for q in range(NQ)] for c in range(4)]
        tri = Tl(tri_s[:])
        onesb = Tl(onesb_s[:])
        mask = Tl(mask_s[:])
        ident = Tl(ident_s[:])
        onesD = Tl(onesD_s[:])
        onesC = Tl(onesC_s[:])
        pc = Tl(pc_s[:])
        nbz = Tl(nbz_s[:])
        ps = [Tl(p[:], "ps%d" % i) for i, p in enumerate(psum)]

        arena_tiles = []
        apos = [0]

        def areset():
            apos[0] = 0

        def carve(nelem, dt, name=""):
            nbytes = nelem * (4 if dt == F32 else 2)
            nbytes = (nbytes + 63) // 64 * 64
            s0 = apos[0]
            assert s0 + nbytes <= ARENA, (name, s0, nbytes)
            apos[0] = s0 + nbytes
            ap = arena[:, s0 // 2: s0 // 2 + (nelem * (2 if dt == F32 else 1))]
            if dt == F32:
                ap = ap.bitcast(F32)
            t = Tl(ap, name)
            t.rng = (s0, s0 + nbytes)
            for o in arena_tiles:
                if o.rng[0] < t.rng[1] and t.rng[0] < o.rng[1]:
                    if o.w is not None:
                        t.r[o.w[0]] = max(t.r.get(o.w[0], 0), o.w[1])
                    for s, v in o.r.items():
                        t.r[s] = max(t.r.get(s, 0), v)
            arena_tiles.append(t)
            return t

        def subtile(parent, c0, n, esz):
            t = Tl(parent.ap[:, c0:c0 + n])
            t.rng = (parent.rng[0] + c0 * esz, parent.rng[0] + (c0 + n) * esz)
            t.r = dict(parent.r)
            arena_tiles.append(t)
            return t

        def emit(eng, fn, reads=(), writes=(), dma=None):
            deps = {}

            def need(src, val):
                if src is eng:
                    return
                if deps.get(src, 0) < val:
                    deps[src] = val

            for t in reads:
                if t.w is not None:
                    need(*t.w)
            for t in writes:
                if t.w is not None:
                    need(*t.w)
                for s, v in t.r.items():
                    need(s, v)
            waits = []
            for s, v in deps.items():
                if eng.seen.get(s, 0) < v:
                    waits.append((s, v))
                    eng.seen[s] = v
            if dma is None:
                eng.count += 1
                tag = (eng, eng.count)
                eng.ops.append((waits, fn, eng, 1))
            else:
                dma.count += 16
                tag = (dma, dma.count)
                eng.ops.append((waits, fn, dma, 16))
            for t in reads:
                t.r[tag[0]] = tag[1]
            for t in writes:
                t.w = tag
                t.r = {}

        def mm(o, l, r, start=True, stop=True):
            emit(PE, lambda e, o=o.ap, l=l.ap, r=r.ap: e.matmul(o, l, r, start=start, stop=stop),
                 reads=[l.t, r.t], writes=[o.t])

        def act(o, i, func, bias=None, scale=1.0):
            rd = [i.t]
            if isinstance(bias, V):
                rd.append(bias.t)
                b = bias.ap
            else:
                b = bias
            sc = scale
            if isinstance(scale, V):
                rd.append(scale.t)
                sc = scale.ap
            if b is None:
                fn = lambda e, o=o.ap, i=i.ap: e.activation(out=o, in_=i, func=func, scale=sc)
            else:
                fn = lambda e, o=o.ap, i=i.ap: e.activation(out=o, in_=i, func=func, bias=b, scale=sc)
            emit(ACT, fn, reads=rd, writes=[o.t])

        def tt(eng, o, a, b, op):
            emit(eng, lambda e, o=o.ap, a=a.ap, b=b.ap: e.tensor_tensor(out=o, in0=a, in1=b, op=op),
                 reads=[a.t, b.t], writes=[o.t])

        def ts(eng, o, a, s1, op0, s2=None, op1=None):
            rd = [a.t]
            s1a = s1
            s2a = s2
            if isinstance(s1, V):
                rd.append(s1.t)
                s1a = s1.ap
            if isinstance(s2, V):
                rd.append(s2.t)
                s2a = s2.ap
            if op1 is None:
                fn = lambda e, o=o.ap, a=a.ap: e.tensor_scalar(out=o, in0=a, scalar1=s1a, scalar2=None, op0=op0)
            else:
                fn = lambda e, o=o.ap, a=a.ap: e.tensor_scalar(out=o, in0=a, scalar1=s1a, scalar2=s2a, op0=op0, op1=op1)
            emit(eng, fn, reads=rd, writes=[o.t])

        def stt(o, a, s, b, op0, op1):
            rd = [a.t, b.t]
            sa = s
            if isinstance(s, V):
                rd.append(s.t)
                sa = s.ap
            emit(DVE, lambda e, o=o.ap, a=a.ap, b=b.ap: e.scalar_tensor_tensor(
                out=o, in0=a, scalar=sa, in1=b, op0=op0, op1=op1), reads=rd, writes=[o.t])

        def cp(eng, o, i):
            emit(eng, lambda e, o=o.ap, i=i.ap: e.tensor_copy(out=o, in_=i), reads=[i.t], writes=[o.t])

        def recip(o, i):
            emit(DVE, lambda e, o=o.ap, i=i.ap: e.reciprocal(out=o, in_=i), reads=[i.t], writes=[o.t])

        def mset(eng, o, val):
            emit(eng, lambda e, o=o.ap: e.memset(o, val), writes=[o.t])

        def dma_in(q, o, src_ap, o_ap=None):
            d = getsem(o.t)
            oa = o.ap if o_ap is None else o_ap
            emit(q, lambda e: e.dma_start(out=oa, in_=src_ap), writes=[o.t], dma=d)

        def dma_out(q, dst_ap, i):
            d = getsem(i.t)
            emit(q, lambda e, i=i.ap: e.dma_start(out=dst_ap, in_=i), reads=[i.t], dma=d)

        def r3(ap, b):
            return ap.rearrange("p (a b) -> p a b", b=b)

        dma_in(POOL, tri, cst_d[:, 0:128])
        dma_in(POOL, mask, cst_d[:, 128:256])
        dma_in(POOL, ident, cst_d[:, 256:384])
        mset(DVE, onesb, 1.0)
        mset(DVE, onesD, 1.0 / 1024.0)
        mset(DVE, onesC, 1.0 / 512.0)
        areset()
        CH = min(S, 2048)
        stage = [carve(CH, F32, "stage%d" % i) for i in range(2)]
        si = 0
        for c in range(8):
            for th in range(S // CH):
                stg = stage[si % 2]
                si += 1
                dma_in(SP, stg, xT[c * 128:(c + 1) * 128, th * CH:(th + 1) * CH])
                for k in range(CH // 512):
                    q = th * (CH // 512) + k
                    act(xhi_t[c][q], stg[:, k * 512:(k + 1) * 512], AF.Copy)
                    tt(DVE, xlo_t[c][q], stg[:, k * 512:(k + 1) * 512], xhi_t[c][q], ALU.subtract)

        out_store_tiles = []

        for l in range(L):
            pb = 0
            dma_in(SP, pc, pcol_d[:, l * NP:(l + 1) * NP])
            ts(DVE, nbz, pc[:, 12:16], -1.0, ALU.mult)

            def bcol(ch, pb=pb):
                return pc[:, pb + ch: pb + ch + 1]

            areset()
            kT = carve(S, BF16, "kT")
            vsb = carve(S, BF16, "vsb")
            WA = carve(4096, BF16, "WA")
            qs_b = [carve(512, BF16, "qs%d" % i) for i in range(1)]
            sz_b = [carve(512, BF16, "sz%d" % i) for i in range(2)]
            bvb = carve(128, F32, "bvb")
            E_b = [carve(512, F32, "E%d" % i) for i in range(3)]
            sp_b = [carve(512, BF16, "sp%d" % i) for i in range(3)]
            S_b = [carve(512, BF16, "S%d" % i) for i in range(2)]
            P_b = [carve(512, F32, "P%d" % i) for i in range(2)]
            w_b = [carve(512, BF16, "w%d" % i) for i in range(2)]
            kT_t = [subtile(kT, q * 512, 512, 2) for q in range(NQ)]
            v_t = [subtile(vsb, kb * 128, 128, 2) for kb in range(S // 128)]
            PJ = [ps[0], ps[1]]
            Zb = [ps[2], ps[3]]
            Cb = [ps[4], ps[5]]
            Ob = [ps[6], ps[7]]
            cnt = {"pj": 0, "E": 0, "sp": 0, "S": 0, "P": 0, "w": 0, "Z": 0, "C": 0, "g": 0}

            def nxt(key, bufs):
                b = bufs[cnt[key] % len(bufs)]
                cnt[key] += 1
                return b

            units = []
            filler = []
            NFcur = [0]
            for hp in range(4):
                def pre_pair(hp=hp):
                    dma_in(POOL, WA, r3(wA[l, hp], 1024), o_ap=r3(WA.ap, 1024))
                    dma_in(SP, bvb, bv_d[l, hp * 128:(hp + 1) * 128].partition_broadcast(128))
                for qt in range(NQ):
                    g = cnt["g"]
                    cnt["g"] += 1
                    qs = qs_b[0]
                    sz = sz_b[g % 2]
                    O = Ob[g % 2]

                    def proj_q(hp=hp, qt=qt, qs=qs):
                        p = nxt("pj", PJ)
                        for c in range(8):
                            mm(p, WA[:, c * 512: c * 512 + 128], xhi_t[c][qt], c == 0, c == 7)
                        act(qs, p, AF.Identity, bias=bcol(hp))

                    def rest_closures(hp, qt, sz):
                        cl = []
                        st = {}
                        for c in range(8):
                            def f(c=c):
                                if c == 0:
                                    st["p"] = nxt("pj", PJ)
                                mm(st["p"], WA[:, c * 512 + 128: c * 512 + 256], xhi_t[c][qt], c == 0, c == 7)
                            cl.append(f)
                        cl.append(lambda: act(kT_t[qt], st["p"], AF.Identity, bias=bcol(4 + hp)))
                        for tb in range(4):
                            for c in range(8):
                                def f(c=c, tb=tb):
                                    if c == 0:
                                        st["p"] = nxt("pj", PJ)
                                    mm(st["p"][:, 0:128], xhi_t[c][qt][:, tb * 128:(tb + 1) * 128],
                                       WA[:, c * 512 + 256: c * 512 + 384], c == 0, c == 7)
                                cl.append(f)
                            cl.append(lambda tb=tb: tt(DVE, v_t[qt * 4 + tb], st["p"][:, 0:128], bvb, ALU.add))
                        for c in range(8):
                            def f(c=c):
                                if c == 0:
                                    st["p"] = nxt("pj", PJ)
                                mm(st["p"], WA[:, c * 512 + 384: c * 512 + 512], xhi_t[c][qt], c == 0, c == 7)
                            cl.append(f)

                        def fz():
                            ee = nxt("P", P_b)
                            act(ee, st["p"], AF.Exp, bias=nbz[:, hp: hp + 1], scale=-1.0)
                            act(ee, ee, AF.Ln, bias=1.0)
                            act(ee, ee, AF.Exp, scale=-1.0)
                            stt(sz, st["p"], bcol(12 + hp), ee, ALU.add, ALU.mult)
                        cl.append(fz)
                        return cl

                    def pre(hp=hp, qt=qt, qs=qs, sz=sz, g=g, pre_pair=pre_pair, proj_q=proj_q,
                            rest_closures=rest_closures):
                        if qt == 0:
                            pre_pair()
                            for f in rest_closures(hp, qt, sz):
                                f()
                        else:
                            while filler:
                                filler.pop(0)()
                        proj_q()
                        if qt + 1 < NQ:
                            filler.extend(rest_closures(hp, qt + 1, sz_b[(g + 1) % 2]))
                            n_g = 2 * (4 * qt + 4)
                            NFcur[0] = -(-len(filler) // max(1, n_g - 3))

                    nb = 4 * qt + 4
                    first_of_group = True
                    for h in range(2):
                        scur = None
                        for i, kb in enumerate(range(nb - 1, -1, -1)):
                            u = dict(h=h, kb=kb, qt=qt, hp=hp, i=i, last=(kb == 0), qs=qs, sz=sz, O=O,
                                     pre=(pre if first_of_group else None),
                                     post=None)
                            first_of_group = False
                            units.append(u)
                    def post(hp=hp, qt=qt, sz=sz, O=O):
                        tt(DVE, gatt_t[hp][qt], O, sz, ALU.mult)
                    units[-1]["post"] = post

            def stageA0(u):
                if u["pre"] is not None:
                    u["pre"]()
                h, kb, qt = u["h"], u["kb"], u["qt"]
                Z = nxt("Z", Zb)
                u["Z"] = Z
                j = kb - 4 * qt
                c0 = 128 * j if j > 0 else 0
                u["c0"] = c0
                mm(Z[:, c0:512], kT_t[kb // 4][64 * h:64 * h + 64, (kb % 4) * 128:(kb % 4) * 128 + 128],
                   u["qs"][64 * h:64 * h + 64, c0:512])

            def stageA1(u):
                h, kb, qt = u["h"], u["kb"], u["qt"]
                Z = u["Z"]
                E = nxt("E", E_b)
                spt = nxt("sp", sp_b)
                u["E"], u["sp"] = E, spt
                j = kb - 4 * qt
                c0 = u["c0"]
                act(E[:, c0:512], Z[:, c0:512], AF.Exp, scale=0.125)
                if j >= 0:
                    tt(DVE, E[:, c0:c0 + 128], E[:, c0:c0 + 128], mask, ALU.mult)
                if c0 > 0:
                    mset(POOL, spt[:, 0:c0], 0.0)
                act(spt[:, c0:512], E[:, c0:512], AF.Ln, bias=1.0)

            chain = {}

            def stageB(u):
                h = u["h"]
                C = nxt("C", Cb)
                P = nxt("P", P_b)
                u["P"] = P
                spt = u["sp"]
                c0 = u["c0"]
                if u["i"] == 0:
                    mm(C[:, c0:512], tri, spt[:, c0:512], True, True)
                    chain[h] = spt
                else:
                    scur = chain[h]
                    mm(C[:, c0:512], tri, spt[:, c0:512], True, False)
                    mm(C[:, c0:512], onesb, scur[:, c0:512], False, True)
                    if not u["last"]:
                        sn = nxt("S", S_b)
                        tt(DVE, sn, scur, spt, ALU.add)
                        chain[h] = sn
                act(P[:, c0:512], C[:, c0:512], AF.Exp, scale=-1.0)
                for _ in range(max(0, NDUM - u.get("nfill", 0))):
                    mm(PJ[0], tri, kT_t[0], True, True)

            def stageC1(u):
                w = nxt("w", w_b)
                u["w"] = w
                c0 = u["c0"]
                if c0 > 0:
                    mset(POOL, w[:, 0:c0], 0.0)
                tt(DVE, w[:, c0:512], u["E"][:, c0:512], u["P"][:, c0:512], ALU.mult)

            def stageC2(u):
                h, kb = u["h"], u["kb"]
                mm(u["O"][64 * h:64 * h + 64, :], v_t[kb][:, 64 * h:64 * h + 64], u["w"], u["i"] == 0, u["last"])
                if u["post"] is not None:
                    u["post"]()

            for hp_ in range(4):
                seg = [u for u in units if u["hp"] == hp_]
                n = len(seg)
                stageA0(seg[0])
                for i in range(n + 2):
                    if 0 <= i - 2 < n:
                        stageC1(seg[i - 2])
                    k_ = 0
                    while k_ < NFcur[0] and filler and not (i + 1 < n and seg[i + 1]["pre"] is not None):
                        filler.pop(0)()
                        k_ += 1
                    if 0 <= i - 1 < n:
                        seg[i - 1]["nfill"] = k_
                    if i + 1 < n:
                        stageA0(seg[i + 1])
                    if i < n:
                        stageA1(seg[i])
                    if 0 <= i - 1 < n:
                        stageB(seg[i - 1])
                    if 0 <= i - 2 < n:
                        stageC2(seg[i - 2])

            if dbg == "gatt":
                for c in range(4):
                    for q in range(NQ):
                        dma_out(POOL, outT[c * 128:(c + 1) * 128, q * 512:(q + 1) * 512], gatt_t[c][q])
                        out_store_tiles.append(gatt_t[c][q])
                break
            areset()
            NW = 4
            WR = [carve(1024, BF16, "WR%d" % i) for i in range(NW)]
            gbuf = [carve(544, BF16, "gbuf%d" % i) for i in range(2)]
            halo = [carve(32, BF16, "halo%d" % i) for i in range(4)]
            ND = 8
            dg = [carve(128, BF16, "dg%d" % i) for i in range(ND)]
            acc = [carve(512, F32, "acc%d" % i) for i in range(4)]
            cg = [carve(512, BF16, "cg%d" % i) for i in range(4)]
            tmp = [carve(512, F32, "tmp%d" % i) for i in range(6)]
            mean_sb = carve(512, F32, "mean")
            rstd = carve(512, F32, "rstd")
            tcnt = [0]

            def ntmp():
                t = tmp[tcnt[0] % len(tmp)]
                tcnt[0] += 1
                return t

            pcnt = [0]

            def nps():
                t = ps[pcnt[0] % 6]
                pcnt[0] += 1
                return t

            dcnt = [0]

            def ndg():
                t = dg[dcnt[0] % ND]
                dcnt[0] += 1
                return t

            slabs = []
            for qt in range(NQ):
                for ch in range(4):
                    slabs.append((wB[l, ch], 1024))
                    slabs.append((wB[l, 4 + ch], 1024))
                for ch in range(4):
                    slabs.append((wB[l, 8 + ch], 1024))
                for j in range(8):
                    slabs.append((wAP[l, j], 512))
                    slabs.append((wB[l, 12 + j], 1024))
                    slabs.append((wB[l, 20 + j], 1024))
                    slabs.append((wCP[l, j], 512))
                for j in range(8):
                    slabs.append((wO[l, j], 1024))
            issued = [0]
            slab_tiles = {}
            PF = 2

            def get_slab(k):
                while issued[0] < min(len(slabs), k + PF + 1):
                    i = issued[0]
                    src, ne = slabs[i]
                    t = WR[i % NW]
                    dma_in(POOL, t, r3(src, 512), o_ap=r3(t.ap[:, 0:ne], 512))
                    slab_tiles[i] = t
                    issued[0] += 1
                return slab_tiles[k]

            sk = [0]

            def next_slab():
                t = get_slab(sk[0])
                sk[0] += 1
                return t

            for hc in range(4):
                mset(DVE, halo[hc], 0.0)

            mean2 = carve(512, F32, "mean2")
            rstd2 = carve(512, F32, "rstd2")

            def ln_stats(pm, pq, mean_o, rstd_o):
                act(mean_o, pm, AF.Identity)
                m2 = ntmp()
                tt(DVE, m2, mean_o, mean_o, ALU.mult)
                var = ntmp()
                tt(DVE, var, pq, m2, ALU.subtract)
                act(var, var, AF.Ln, bias=LN_EPS)
                act(rstd_o, var, AF.Exp, scale=-0.5)

            pm = ps[6]
            pq = ps[7]

            def stage1(qt, extra):
                def PP(ch):
                    sa = next_slab()
                    pa = nps()
                    for c in range(8):
                        mm(pa, sa[:, c * 128:(c + 1) * 128], xhi_t[c][qt], c == 0, c == 7)
                    sbw = next_slab()
                    pbp = nps()
                    for c in range(8):
                        mm(pbp, sbw[:, c * 128:(c + 1) * 128], xhi_t[c][qt], c == 0, c == 7)
                    return pa, pbp

                def GLUCONV(ch, pa, pbp):
                    sg = ntmp()
                    act(sg, pbp, AF.Sigmoid, bias=bcol(20 + ch))
                    gb = gbuf[ch % 2]
                    cp(DVE, gb[:, 0:30], halo[ch][:, 0:30])
                    stt(gb[:, 30:542], pa, bcol(16 + ch), sg, ALU.add, ALU.mult)
                    pcv = nps()
                    wc0 = pb + 44 + ch * 31
                    for k in range(31):
                        d = ndg()
                        ts(DVE, d, ident, pc[:, wc0 + k: wc0 + k + 1], ALU.mult)
                        mm(pcv, d, gb[:, k:k + 512], k == 0, k == 30)
                    act(acc[ch], pcv, AF.Identity, bias=pc[:, pb + 168 + ch: pb + 169 + ch])
                    cp(POOL, halo[ch][:, 0:30], gb[:, 512:542])
                    mm(pm, onesC, acc[ch], ch == 0, ch == 3)
                    sq = ntmp()
                    act(sq, acc[ch], AF.Square)
                    mm(pq, onesC, sq, ch == 0, ch == 3)
                    for _ in range(2):
                        if extra:
                            extra.pop(0)()

                pp = {0: PP(0)}
                for ch in range(4):
                    if ch + 1 < 4:
                        pp[ch + 1] = PP(ch + 1)
                    GLUCONV(ch, *pp[ch])
                pzs = {}
                for ch in range(2):
                    sz_ = next_slab()
                    pz = nps()
                    for c in range(8):
                        mm(pz, sz_[:, c * 128:(c + 1) * 128], xhi_t[c][qt], c == 0, c == 7)
                    pzs[ch] = pz
                ln_stats(pm, pq, mean_sb, rstd)
                for ch in range(4):
                    if ch in pzs:
                        pz = pzs[ch]
                    else:
                        sz_ = next_slab()
                        pz = nps()
                        for c in range(8):
                            mm(pz, sz_[:, c * 128:(c + 1) * 128], xhi_t[c][qt], c == 0, c == 7)
                    cn = ntmp()
                    tt(DVE, cn, acc[ch], mean_sb, ALU.subtract)
                    tt(DVE, cn, cn, rstd, ALU.mult)
                    gcol = pc[:, pb + 172 + ch: pb + 173 + ch]
                    bcl = pc[:, pb + 176 + ch: pb + 177 + ch]
                    lno = ntmp()
                    act(lno, cn, AF.Identity, bias=bcl, scale=gcol)
                    sg = ntmp()
                    act(sg, cn, AF.Sigmoid, bias=bcl, scale=gcol)
                    tt(DVE, lno, lno, sg, ALU.mult)
                    sgz = ntmp()
                    act(sgz, pz, AF.Sigmoid, bias=bcol(24 + ch))
                    zc = ntmp()
                    stt(zc, pz, bcol(24 + ch), sgz, ALU.add, ALU.mult)
                    tt(DVE, cg[ch], lno, zc, ALU.mult)
                while extra:
                    extra.pop(0)()

            def stage23(qt):
                nonlocal_acc = acc_box
                save = apos[0]
                apos[0] = nonlocal_acc[0][0].rng[0]
                merged = [carve(512, BF16, "mg%d" % i) for i in range(8)]
                apos[0] = save
                for j in range(8):
                    s_ap = next_slab()
                    pab = nps()
                    for c in range(4):
                        mm(pab, s_ap[:, c * 128:(c + 1) * 128], gatt_t[c][qt], c == 0, c == 3)
                    s_ga = next_slab()
                    pga = nps()
                    for c in range(8):
                        mm(pga, s_ga[:, c * 128:(c + 1) * 128], xhi_t[c][qt], c == 0, c == 7)
                    s_gc = next_slab()
                    pgc = nps()
                    for c in range(8):
                        mm(pgc, s_gc[:, c * 128:(c + 1) * 128], xhi_t[c][qt], c == 0, c == 7)
                    s_cp = next_slab()
                    pcb = nps()
                    for c in range(4):
                        mm(pcb, s_cp[:, c * 128:(c + 1) * 128], cg[c], c == 0, c == 3)
                    sga = ntmp()
                    act(sga, pga, AF.Sigmoid, bias=bcol(28 + j))
                    sgc = ntmp()
                    act(sgc, pgc, AF.Sigmoid, bias=bcol(36 + j))
                    m1 = ntmp()
                    tt(DVE, m1, pab, sga, ALU.mult)
                    m2_ = ntmp()
                    stt(m2_, pcb, pc[:, pb + 180 + j: pb + 181 + j], sgc, ALU.add, ALU.mult)
                    tt(DVE, merged[j], m1, m2_, ALU.add)
                save = apos[0]
                apos[0] = nonlocal_acc[0][0].rng[0]
                nonlocal_acc[0] = [carve(512, F32, "acc%d" % i) for i in range(4)]
                apos[0] = save
                for j in range(8):
                    s_o = next_slab()
                    py = nps()
                    for c in range(8):
                        mm(py, s_o[:, c * 128:(c + 1) * 128], merged[c], c == 0, c == 7)
                    xs = ntmp()
                    tt(DVE, xs, xhi_t[j][qt], xlo_t[j][qt], ALU.add)
                    r = ntmp()
                    stt(r, xs, ALPHA, py, ALU.mult, ALU.add)
                    sq = ntmp()
                    act(sq, r, AF.Square)
                    mm(pm, onesD, r, j == 0, j == 7)
                    mm(pq, onesD, sq, j == 0, j == 7)
                    act(xhi_t[j][qt], r, AF.Copy)
                    tt(DVE, xlo_t[j][qt], r, xhi_t[j][qt], ALU.subtract)
                ln_stats(pm, pq, mean2, rstd2)

            def norm_chunk(qt, j):
                r = ntmp()
                tt(DVE, r, xhi_t[j][qt], xlo_t[j][qt], ALU.add)
                tt(DVE, r, r, mean2, ALU.subtract)
                tt(DVE, r, r, rstd2, ALU.mult)
                xn = ntmp()
                act(xn, r, AF.Identity, bias=pc[:, pb + 196 + j: pb + 197 + j],
                    scale=pc[:, pb + 188 + j: pb + 189 + j])
                if l == L - 1:
                    dma_out(SP, outT[j * 128:(j + 1) * 128, qt * 512:(qt + 1) * 512], xn)
                    if xn not in out_store_tiles:
                        out_store_tiles.append(xn)
                else:
                    act(xhi_t[j][qt], xn, AF.Copy)
                    tt(DVE, xlo_t[j][qt], xn, xhi_t[j][qt], ALU.subtract)

            acc_box = [acc]
            pending = []
            for qt in range(NQ):
                acc = acc_box[0]
                stage1(qt, pending)
                stage23(qt)
                pending = [(lambda qt=qt, j=j: norm_chunk(qt, j)) for j in range(8)]
            while pending:
                pending.pop(0)()

        fin_waits = [(t.dsem, t.dsem.count) for t in out_store_tiles]
        SP.ops.append((fin_waits, None, None, 0))

        def replay(eng, e):
            for waits, fn, src, inc in eng.ops:
                for s, v in waits:
                    e.wait_ge(s.sem, v)
                if fn is not None:
                    fn(e).then_inc(src.sem, inc)

        with nc.Block() as block:
            @block.tensor
            def _(e):
                replay(PE, e)

            @block.scalar
            def _(e):
                replay(ACT, e)

            @block.vector
            def _(e):
                replay(DVE, e)

            @block.gpsimd
            def _(e):
                replay(POOL, e)

            @block.sync
            def _(e):
                replay(SP, e)
    return nc


def _consts():
    j = np.arange(128)[:, None]
    s = np.arange(128)[None, :]
    tri = (j >= s).astype(np.float32)
    msk = (j < s).astype(np.float32)
    return np.ascontiguousarray(np.concatenate([tri, msk, np.eye(128, dtype=np.float32)], axis=1))


def _prep_weights(inp, layers):
    L = len(layers)
    f = lambda a: np.ascontiguousarray(a, dtype=np.float32)
    w_in = np.asarray(inp["w_in"])[layers]
    wA = np.empty((L, 4, 128, 8, 512), np.float32)
    w4 = w_in.reshape(L, 8, 128, 5632)
    for hp in range(4):
        for j in range(4):
            blk = w4[:, :, :, j * 512 + hp * 128: j * 512 + hp * 128 + 128]
            wA[:, hp, :, :, j * 128:(j + 1) * 128] = blk.transpose(0, 2, 1, 3)
    wB = w4[:, :, :, 2048:].reshape(L, 8, 128, 28, 128).transpose(0, 3, 2, 1, 4)
    wAP = np.asarray(inp["w_att_proj"])[layers].reshape(L, 4, 128, 8, 128).transpose(0, 3, 2, 1, 4)
    wCP = np.asarray(inp["w_conv_proj"])[layers].reshape(L, 4, 128, 8, 128).transpose(0, 3, 2, 1, 4)
    wO = np.asarray(inp["w_out"])[layers].reshape(L, 8, 128, 8, 128).transpose(0, 3, 2, 1, 4)
    pcol = np.empty((128, L, NP), np.float32)
    b_in = np.asarray(inp["b_in"])[layers]
    pcol[:, :, 0:44] = b_in.reshape(L, 44, 128).transpose(2, 0, 1)
    cw = np.asarray(inp["conv_w"])[layers]
    pcol[:, :, 44:168] = cw.reshape(L, 31, 4, 128).transpose(3, 0, 2, 1).reshape(128, L, 124)
    def col(name, n):
        return np.asarray(inp[name])[layers].reshape(L, n, 128).transpose(2, 0, 1)
    pcol[:, :, 168:172] = col("conv_b", 4)
    pcol[:, :, 172:176] = col("conv_ln_g", 4)
    pcol[:, :, 176:180] = col("conv_ln_b", 4)
    pcol[:, :, 180:188] = col("b_conv_proj", 8)
    pcol[:, :, 188:196] = col("ln_g", 8)
    pcol[:, :, 196:204] = col("ln_b", 8)
    return {
        "wA": f(wA.reshape(L, 4, 128, 4096)),
        "wB": f(wB.reshape(L, 28, 128, 1024)),
        "wAP": f(wAP.reshape(L, 8, 128, 512)),
        "wCP": f(wCP.reshape(L, 8, 128, 512)),
        "wO": f(wO.reshape(L, 8, 128, 1024)),
        "pcol": f(pcol.reshape(128, L * NP)),
        "bv": f(b_in[:, 1024:1536]),
        "cst": _consts(),
    }


_NC_CACHE = {}


def _get_nc(S, L):
    if (S, L) not in _NC_CACHE:
        _NC_CACHE[(S, L)] = build(S, L)
    return _NC_CACHE[(S, L)]


def run_layers(xTs, inp, layers):
    S = xTs[0].shape[1]
    nc = _get_nc(S, len(layers))
    wts = _prep_weights(inp, layers)
    in_maps = [dict(wts, xT=np.ascontiguousarray(x)) for x in xTs]
    res = run_bass_kernel_spmd(nc, in_maps, core_ids=list(range(len(xTs))))
    return [r["outT"] for r in res.results]


def kernel(x, w_in, b_in, conv_w, conv_b, conv_ln_g, conv_ln_b,
           w_att_proj, w_conv_proj, b_conv_proj, w_out, ln_g, ln_b):
    inp = dict(w_in=w_in, b_in=b_in, conv_w=conv_w, conv_b=conv_b, conv_ln_g=conv_ln_g,
               conv_ln_b=conv_ln_b, w_att_proj=w_att_proj, w_conv_proj=w_conv_proj,
               b_conv_proj=b_conv_proj, w_out=w_out, ln_g=ln_g, ln_b=ln_b)
    x = np.asarray(x, dtype=np.float32)
    B = x.shape[0]
    xTs = [np.ascontiguousarray(x[b].T) for b in range(B)]
    if MODE == "fused":
        xTs = run_layers(xTs, inp, list(range(DEPTH)))
    else:
        for l in range(DEPTH):
            xTs = run_layers(xTs, inp, [l])
    return np.ascontiguousarray(np.stack([o.T for o in xTs], axis=0)).astype(np.float32)
```

```python
from contextlib import ExitStack

import numpy as np
import concourse.bass as bass
import concourse.mybir as mybir
from concourse.bass_utils import run_bass_kernel_spmd

F32 = mybir.dt.float32
BF16 = mybir.dt.bfloat16
AF = mybir.ActivationFunctionType
ALU = mybir.AluOpType

D = 1024
DEPTH = 4
SEQ = 4096
NP = 204
ALPHA = float((2 * DEPTH) ** 0.25)
LN_EPS = 1e-5
NDUM = 1
MODE = "fused"


class Src:
    def __init__(self, name):
        self.name = name
        self.sem = None
        self.count = 0


class Eng(Src):
    def __init__(self, name):
        super().__init__(name)
        self.ops = []
        self.seen = {}


class V:
    def __init__(self, t, ap):
        self.t = t
        self.ap = ap


class Tl(V):
    def __init__(self, ap, name=""):
        self.t = self
        self.ap = ap
        self.w = None
        self.r = {}
        self.name = name
        self.dsem = None
        self.rng = None

    def __getitem__(self, k):
        return V(self, self.ap[k])


def build(S, L, dbg=None):
    NQ = S // 512
    nc = bass.Bass("TRN2", target_bir_lowering=False)
    xT = nc.dram_tensor("xT", [D, S], F32, kind="ExternalInput").ap()
    wA = nc.dram_tensor("wA", [L, 4, 128, 4096], F32, kind="ExternalInput").ap()
    wB = nc.dram_tensor("wB", [L, 28, 128, 1024], F32, kind="ExternalInput").ap()
    wAP = nc.dram_tensor("wAP", [L, 8, 128, 512], F32, kind="ExternalInput").ap()
    wCP = nc.dram_tensor("wCP", [L, 8, 128, 512], F32, kind="ExternalInput").ap()
    wO = nc.dram_tensor("wO", [L, 8, 128, 1024], F32, kind="ExternalInput").ap()
    pcol_d = nc.dram_tensor("pcol", [128, L * NP], F32, kind="ExternalInput").ap()
    bv_d = nc.dram_tensor("bv", [L, 512], F32, kind="ExternalInput").ap()
    cst_d = nc.dram_tensor("cst", [128, 384], F32, kind="ExternalInput").ap()
    outT = nc.dram_tensor("outT", [D, S], F32, kind="ExternalOutput").ap()

    PE, ACT, DVE, POOL, SP = Eng("pe"), Eng("act"), Eng("dve"), Eng("pool"), Eng("sp")
    engines = [PE, ACT, DVE, POOL, SP]
    dsems = []

    es = ExitStack()
    with es:
        def sb(name, shape, dt):
            return es.enter_context(nc.sbuf_tensor(name, shape, dt))

        for e in engines:
            e.sem = es.enter_context(nc.semaphore("s_" + e.name))

        def getsem(t):
            if t.dsem is None:
                s = Src("d%d" % len(dsems))
                s.sem = es.enter_context(nc.semaphore(s.name))
                dsems.append(s)
                t.dsem = s
            return t.dsem

        xhi = sb("xhi", [128, 8 * S], BF16)
        xlo = sb("xlo", [128, 8 * S], BF16)
        gatt = sb("gatt", [128, 4 * S], BF16)
        tri_s = sb("tri", [128, 128], BF16)
        onesb_s = sb("onesb", [128, 128], BF16)
        mask_s = sb("mask", [128, 128], BF16)
        ident_s = sb("ident", [128, 128], BF16)
        onesD_s = sb("onesD", [128, 128], F32)
        onesC_s = sb("onesC", [128, 128], F32)
        pc_s = sb("pc", [128, NP], F32)
        nbz_s = sb("nbz", [128, 4], F32)
        ARENA = 45568
        arena = sb("arena", [128, ARENA // 2], BF16)
        psum = [es.enter_context(nc.psum_tensor("ps%d" % i, [128, 512], F32)) for i in range(8)]

        xhi_t = [[Tl(xhi[:, c * S + q * 512: c * S + (q + 1) * 512]) for q in range(NQ)] for c in range(8)]
        xlo_t = [[Tl(xlo[:, c * S + q * 512: c * S + (q + 1) * 512]) for q in range(NQ)] for c in range(8)]
        gatt_t = [[Tl(gatt[:, c * S + q * 512: c * S + (q + 1) * 512]) for q in range(NQ)] for c in range(4)]
        tri = Tl(tri_s[:])
        onesb = Tl(onesb_s[:])
        mask = Tl(mask_s[:])
        ident = Tl(ident_s[:])
        onesD = Tl(onesD_s[:])
        onesC = Tl(onesC_s[:])
        pc = Tl(pc_s[:])
        nbz = Tl(nbz_s[:])
        ps = [Tl(p[:], "ps%d" % i) for i, p in enumerate(psum)]

        arena_tiles = []
        apos = [0]

        def areset():
            apos[0] = 0

        def carve(nelem, dt, name=""):
            nbytes = nelem * (4 if dt == F32 else 2)
            nbytes = (nbytes + 63) // 64 * 64
            s0 = apos[0]
            assert s0 + nbytes <= ARENA, (name, s0, nbytes)
            apos[0] = s0 + nbytes
            ap = arena[:, s0 // 2: s0 // 2 + (nelem * (2 if dt == F32 else 1))]
            if dt == F32:
                ap = ap.bitcast(F32)
            t = Tl(ap, name)
            t.rng = (s0, s0 + nbytes)
            for o in arena_tiles:
                if o.rng[0] < t.rng[1] and t.rng[0] < o.rng[1]:
                    if o.w is not None:
                        t.r[o.w[0]] = max(t.r.get(o.w[0], 0), o.w[1])
                    for s, v in o.r.items():
                        t.r[s] = max(t.r.get(s, 0), v)
            arena_tiles.append(t)
            return t

        def subtile(parent, c0, n, esz):
            t = Tl(parent.ap[:, c0:c0 + n])
            t.rng = (parent.rng[0] + c0 * esz, parent.rng[0] + (c0 + n) * esz)
            t.r = dict(parent.r)
            arena_tiles.append(t)
            return t

        def emit(eng, fn, reads=(), writes=(), dma=None):
            deps = {}

            def need(src, val):
                if src is eng:
                    return
                if deps.get(src, 0) < val:
                    deps[src] = val

            for t in reads:
                if t.w is not None:
                    need(*t.w)
            for t in writes:
                if t.w is not None:
                    need(*t.w)
                for s, v in t.r.items():
                    need(s, v)
            waits = []
            for s, v in deps.items():
                if eng.seen.get(s, 0) < v:
                    waits.append((s, v))
                    eng.seen[s] = v
            if dma is None:
                eng.count += 1
                tag = (eng, eng.count)
                eng.ops.append((waits, fn, eng, 1))
            else:
                dma.count += 16
                tag = (dma, dma.count)
                eng.ops.append((waits, fn, dma, 16))
            for t in reads:
                t.r[tag[0]] = tag[1]
            for t in writes:
                t.w = tag
                t.r = {}

        def mm(o, l, r, start=True, stop=True):
            emit(PE, lambda e, o=o.ap, l=l.ap, r=r.ap: e.matmul(o, l, r, start=start, stop=stop),
                 reads=[l.t, r.t], writes=[o.t])

        def act(o, i, func, bias=None, scale=1.0):
            rd = [i.t]
            if isinstance(bias, V):
                rd.append(bias.t)
                b = bias.ap
            else:
                b = bias
            sc = scale
            if isinstance(scale, V):
                rd.append(scale.t)
                sc = scale.ap
            if b is None:
                fn = lambda e, o=o.ap, i=i.ap: e.activation(out=o, in_=i, func=func, scale=sc)
            else:
                fn = lambda e, o=o.ap, i=i.ap: e.activation(out=o, in_=i, func=func, bias=b, scale=sc)
            emit(ACT, fn, reads=rd, writes=[o.t])

        def tt(eng, o, a, b, op):
            emit(eng, lambda e, o=o.ap, a=a.ap, b=b.ap: e.tensor_tensor(out=o, in0=a, in1=b, op=op),
                 reads=[a.t, b.t], writes=[o.t])

        def ts(eng, o, a, s1, op0, s2=None, op1=None):
            rd = [a.t]
            s1a = s1
            s2a = s2
            if isinstance(s1, V):
                rd.append(s1.t)
                s1a = s1.ap
            if isinstance(s2, V):
                rd.append(s2.t)
                s2a = s2.ap
            if op1 is None:
                fn = lambda e, o=o.ap, a=a.ap: e.tensor_scalar(out=o, in0=a, scalar1=s1a, scalar2=None, op0=op0)
            else:
                fn = lambda e, o=o.ap, a=a.ap: e.tensor_scalar(out=o, in0=a, scalar1=s1a, scalar2=s2a, op0=op0, op1=op1)
            emit(eng, fn, reads=rd, writes=[o.t])

        def stt(o, a, s, b, op0, op1):
            rd = [a.t, b.t]
            sa = s
            if isinstance(s, V):
                rd.append(s.t)
                sa = s.ap
            emit(DVE, lambda e, o=o.ap, a=a.ap, b=b.ap: e.scalar_tensor_tensor(
                out=o, in0=a, scalar=sa, in1=b, op0=op0, op1=op1), reads=rd, writes=[o.t])

        def cp(eng, o, i):
            emit(eng, lambda e, o=o.ap, i=i.ap: e.tensor_copy(out=o, in_=i), reads=[i.t], writes=[o.t])

        def recip(o, i):
            emit(DVE, lambda e, o=o.ap, i=i.ap: e.reciprocal(out=o, in_=i), reads=[i.t], writes=[o.t])

        def mset(eng, o, val):
            emit(eng, lambda e, o=o.ap: e.memset(o, val), writes=[o.t])

        def dma_in(q, o, src_ap, o_ap=None):
            d = getsem(o.t)
            oa = o.ap if o_ap is None else o_ap
            emit(q, lambda e: e.dma_start(out=oa, in_=src_ap), writes=[o.t], dma=d)

        def dma_out(q, dst_ap, i):
            d = getsem(i.t)
            emit(q, lambda e, i=i.ap: e.dma_start(out=dst_ap, in_=i), reads=[i.t], dma=d)

        def r3(ap, b):
            return ap.rearrange("p (a b) -> p a b", b=b)

        dma_in(POOL, tri, cst_d[:, 0:128])
        dma_in(POOL, mask, cst_d[:, 128:256])
        dma_in(POOL, ident, cst_d[:, 256:384])
        mset(DVE, onesb, 1.0)
        mset(DVE, onesD, 1.0 / 1024.0)
        mset(DVE, onesC, 1.0 / 512.0)
        areset()
        CH = min(S, 2048)
        stage = [carve(CH, F32, "stage%d" % i) for i in range(2)]
        si = 0
        for c in range(8):
            for th in range(S // CH):
                stg = stage[si % 2]
                si += 1
                dma_in(SP, stg, xT[c * 128:(c + 1) * 128, th * CH:(th + 1) * CH])
                for k in range(CH // 512):
                    q = th * (CH // 512) + k
                    act(xhi_t[c][q], stg[:, k * 512:(k + 1) * 512], AF.Copy)
                    tt(DVE, xlo_t[c][q], stg[:, k * 512:(k + 1) * 512], xhi_t[c][q], ALU.subtract)

        out_store_tiles = []

        for l in range(L):
            pb = 0
            dma_in(SP, pc, pcol_d[:, l * NP:(l + 1) * NP])
            ts(DVE, nbz, pc[:, 12:16], -1.0, ALU.mult)

            def bcol(ch, pb=pb):
                return pc[:, pb + ch: pb + ch + 1]

            areset()
            kT = carve(S, BF16, "kT")
            vsb = carve(S, BF16, "vsb")
            WA = carve(4096, BF16, "WA")
            qs_b = [carve(512, BF16, "qs%d" % i) for i in range(1)]
            sz_b = [carve(512, BF16, "sz%d" % i) for i in range(2)]
            bvb = carve(128, F32, "bvb")
            E_b = [carve(512, F32, "E%d" % i) for i in range(3)]
            sp_b = [carve(512, BF16, "sp%d" % i) for i in range(3)]
            S_b = [carve(512, BF16, "S%d" % i) for i in range(2)]
            P_b = [carve(512, F32, "P%d" % i) for i in range(2)]
            w_b = [carve(512, BF16, "w%d" % i) for i in range(2)]
            kT_t = [subtile(kT, q * 512, 512, 2) for q in range(NQ)]
            v_t = [subtile(vsb, kb * 128, 128, 2) for kb in range(S // 128)]
            PJ = [ps[0], ps[1]]
            Zb = [ps[2], ps[3]]
            Cb = [ps[4], ps[5]]
            Ob = [ps[6], ps[7]]
            cnt = {"pj": 0, "E": 0, "sp": 0, "S": 0, "P": 0, "w": 0, "Z": 0, "C": 0, "g": 0}

            def nxt(key, bufs):
                b = bufs[cnt[key] % len(bufs)]
                cnt[key] += 1
                return b

            units = []
            filler = []
            NFcur = [0]
            for hp in range(4):
                def pre_pair(hp=hp):
                    dma_in(POOL, WA, r3(wA[l, hp], 1024), o_ap=r3(WA.ap, 1024))
                    dma_in(SP, bvb, bv_d[l, hp * 128:(hp + 1) * 128].partition_broadcast(128))
                for qt in range(NQ):
                    g = cnt["g"]
                    cnt["g"] += 1
                    qs = qs_b[0]
                    sz = sz_b[g % 2]
                    O = Ob[g % 2]

                    def proj_q(hp=hp, qt=qt, qs=qs):
                        p = nxt("pj", PJ)
                        for c in range(8):
                            mm(p, WA[:, c * 512: c * 512 + 128], xhi_t[c][qt], c == 0, c == 7)
                        act(qs, p, AF.Identity, bias=bcol(hp))

                    def rest_closures(hp, qt, sz):
                        cl = []
                        st = {}
                        for c in range(8):
                            def f(c=c):
                                if c == 0:
                                    st["p"] = nxt("pj", PJ)
                                mm(st["p"], WA[:, c * 512 + 128: c * 512 + 256], xhi_t[c][qt], c == 0, c == 7)
                            cl.append(f)
                        cl.append(lambda: act(kT_t[qt], st["p"], AF.Identity, bias=bcol(4 + hp)))
                        for tb in range(4):
                            for c in range(8):
                                def f(c=c, tb=tb):
                                    if c == 0:
                                        st["p"] = nxt("pj", PJ)
                                    mm(st["p"][:, 0:128], xhi_t[c][qt][:, tb * 128:(tb + 1) * 128],
                                       WA[:, c * 512 + 256: c * 512 + 384], c == 0, c == 7)
                                cl.append(f)
                            cl.append(lambda tb=tb: tt(DVE, v_t[qt * 4 + tb], st["p"][:, 0:128], bvb, ALU.add))
                        for c in range(8):
                            def f(c=c):
                                if c == 0:
                                    st["p"] = nxt("pj", PJ)
                                mm(st["p"], WA[:, c * 512 + 384: c * 512 + 512], xhi_t[c][qt], c == 0, c == 7)
                            cl.append(f)

                        def fz():
                            ee = nxt("P", P_b)
                            act(ee, st["p"], AF.Exp, bias=nbz[:, hp: hp + 1], scale=-1.0)
                            act(ee, ee, AF.Ln, bias=1.0)
                            act(ee, ee, AF.Exp, scale=-1.0)
                            stt(sz, st["p"], bcol(12 + hp), ee, ALU.add, ALU.mult)
                        cl.append(fz)
                        return cl

                    def pre(hp=hp, qt=qt, qs=qs, sz=sz, g=g, pre_pair=pre_pair, proj_q=proj_q,
                            rest_closures=rest_closures):
                        if qt == 0:
                            pre_pair()
                            for f in rest_closures(hp, qt, sz):
                                f()
                        else:
                            while filler:
                                filler.pop(0)()
                        proj_q()
                        if qt + 1 < NQ:
                            filler.extend(rest_closures(hp, qt + 1, sz_b[(g + 1) % 2]))
                            n_g = 2 * (4 * qt + 4)
                            NFcur[0] = -(-len(filler) // max(1, n_g - 3))

                    nb = 4 * qt + 4
                    first_of_group = True
                    for h in range(2):
                        scur = None
                        for i, kb in enumerate(range(nb - 1, -1, -1)):
                            u = dict(h=h, kb=kb, qt=qt, hp=hp, i=i, last=(kb == 0), qs=qs, sz=sz, O=O,
                                     pre=(pre if first_of_group else None),
                                     post=None)
                            first_of_group = False
                            units.append(u)
                    def post(hp=hp, qt=qt, sz=sz, O=O):
                        tt(DVE, gatt_t[hp][qt], O, sz, ALU.mult)
                    units[-1]["post"] = post

            def stageA0(u):
                if u["pre"] is not None:
                    u["pre"]()
                h, kb, qt = u["h"], u["kb"], u["qt"]
                Z = nxt("Z", Zb)
                u["Z"] = Z
                j = kb - 4 * qt
                c0 = 128 * j if j > 0 else 0
                u["c0"] = c0
                mm(Z[:, c0:512], kT_t[kb // 4][64 * h:64 * h + 64, (kb % 4) * 128:(kb % 4) * 128 + 128],
                   u["qs"][64 * h:64 * h + 64, c0:512])

            def stageA1(u):
                h, kb, qt = u["h"], u["kb"], u["qt"]
                Z = u["Z"]
                E = nxt("E", E_b)
                spt = nxt("sp", sp_b)
                u["E"], u["sp"] = E, spt
                j = kb - 4 * qt
                c0 = u["c0"]
                act(E[:, c0:512], Z[:, c0:512], AF.Exp, scale=0.125)
                if j >= 0:
                    tt(DVE, E[:, c0:c0 + 128], E[:, c0:c0 + 128], mask, ALU.mult)
                if c0 > 0:
                    mset(POOL, spt[:, 0:c0], 0.0)
                act(spt[:, c0:512], E[:, c0:512], AF.Ln, bias=1.0)

            chain = {}

            def stageB(u):
                h = u["h"]
                C = nxt("C", Cb)
                P = nxt("P", P_b)
                u["P"] = P
                spt = u["sp"]
                c0 = u["c0"]
                if u["i"] == 0:
                    mm(C[:, c0:512], tri, spt[:, c0:512], True, True)
                    chain[h] = spt
                else:
                    scur = chain[h]
                    mm(C[:, c0:512], tri, spt[:, c0:512], True, False)
                    mm(C[:, c0:512], onesb, scur[:, c0:512], False, True)
                    if not u["last"]:
                        sn = nxt("S", S_b)
                        tt(DVE, sn, scur, spt, ALU.add)
                        chain[h] = sn
                act(P[:, c0:512], C[:, c0:512], AF.Exp, scale=-1.0)
                for _ in range(max(0, NDUM - u.get("nfill", 0))):
                    mm(PJ[0], tri, kT_t[0], True, True)

            def stageC1(u):
                w = nxt("w", w_b)
                u["w"] = w
                c0 = u["c0"]
                if c0 > 0:
                    mset(POOL, w[:, 0:c0], 0.0)
                tt(DVE, w[:, c0:512], u["E"][:, c0:512], u["P"][:, c0:512], ALU.mult)

            def stageC2(u):
                h, kb = u["h"], u["kb"]
                mm(u["O"][64 * h:64 * h + 64, :], v_t[kb][:, 64 * h:64 * h + 64], u["w"], u["i"] == 0, u["last"])
                if u["post"] is not None:
                    u["post"]()

            for hp_ in range(4):
                seg = [u for u in units if u["hp"] == hp_]
                n = len(seg)
                stageA0(seg[0])
                for i in range(n + 2):
                    if 0 <= i - 2 < n:
                        stageC1(seg[i - 2])
                    k_ = 0
                    while k_ < NFcur[0] and filler and not (i + 1 < n and seg[i + 1]["pre"] is not None):
                        filler.pop(0)()
                        k_ += 1
                    if 0 <= i - 1 < n:
                        seg[i - 1]["nfill"] = k_
                    if i + 1 < n:
                        stageA0(seg[i + 1])
                    if i < n:
                        stageA1(seg[i])
                    if 0 <= i - 1 < n:
                        stageB(seg[i - 1])
                    if 0 <= i - 2 < n:
                        stageC2(seg[i - 2])

            if dbg == "gatt":
                for c in range(4):
                    for q in range(NQ):
                        dma_out(POOL, outT[c * 128:(c + 1) * 128, q * 512:(q + 1) * 512], gatt_t[c][q])
                        out_store_tiles.append(gatt_t[c][q])
                break
            areset()
            NW = 4
            WR = [carve(1024, BF16, "WR%d" % i) for i in range(NW)]
            gbuf = [carve(544, BF16, "gbuf%d" % i) for i in range(2)]
            halo = [carve(32, BF16, "halo%d" % i) for i in range(4)]
            ND = 8
            dg = [carve(128, BF16, "dg%d" % i) for i in range(ND)]
            acc = [carve(512, F32, "acc%d" % i) for i in range(4)]
            cg = [carve(512, BF16, "cg%d" % i) for i in range(4)]
            tmp = [carve(512, F32, "tmp%d" % i) for i in range(6)]
            mean_sb = carve(512, F32, "mean")
            rstd = carve(512, F32, "rstd")
            tcnt = [0]

            def ntmp():
                t = tmp[tcnt[0] % len(tmp)]
                tcnt[0] += 1
                return t

            pcnt = [0]

            def nps():
                t = ps[pcnt[0] % 6]
                pcnt[0] += 1
                return t

            dcnt = [0]

            def ndg():
                t = dg[dcnt[0] % ND]
                dcnt[0] += 1
                return t

            slabs = []
            for qt in range(NQ):
                for ch in range(4):
                    slabs.append((wB[l, ch], 1024))
                    slabs.append((wB[l, 4 + ch], 1024))
                for ch in range(4):
                    slabs.append((wB[l, 8 + ch], 1024))
                for j in range(8):
                    slabs.append((wAP[l, j], 512))
                    slabs.append((wB[l, 12 + j], 1024))
                    slabs.append((wB[l, 20 + j], 1024))
                    slabs.append((wCP[l, j], 512))
                for j in range(8):
                    slabs.append((wO[l, j], 1024))
            issued = [0]
            slab_tiles = {}
            PF = 2

            def get_slab(k):
                while issued[0] < min(len(slabs), k + PF + 1):
                    i = issued[0]
                    src, ne = slabs[i]
                    t = WR[i % NW]
                    dma_in(POOL, t, r3(src, 512), o_ap=r3(t.ap[:, 0:ne], 512))
                    slab_tiles[i] = t
                    issued[0] += 1
                return slab_tiles[k]

            sk = [0]

            def next_slab():
                t = get_slab(sk[0])
                sk[0] += 1
                return t

            for hc in range(4):
                mset(DVE, halo[hc], 0.0)

            mean2 = carve(512, F32, "mean2")
            rstd2 = carve(512, F32, "rstd2")

            def ln_stats(pm, pq, mean_o, rstd_o):
                act(mean_o, pm, AF.Identity)
                m2 = ntmp()
                tt(DVE, m2, mean_o, mean_o, ALU.mult)
                var = ntmp()
                tt(DVE, var, pq, m2, ALU.subtract)
                act(var, var, AF.Ln, bias=LN_EPS)
                act(rstd_o, var, AF.Exp, scale=-0.5)

            pm = ps[6]
            pq = ps[7]

            def stage1(qt, extra):
                def PP(ch):
                    sa = next_slab()
                    pa = nps()
                    for c in range(8):
                        mm(pa, sa[:, c * 128:(c + 1) * 128], xhi_t[c][qt], c == 0, c == 7)
                    sbw = next_slab()
                    pbp = nps()
                    for c in range(8):
                        mm(pbp, sbw[:, c * 128:(c + 1) * 128], xhi_t[c][qt], c == 0, c == 7)
                    return pa, pbp

                def GLUCONV(ch, pa, pbp):
                    sg = ntmp()
                    act(sg, pbp, AF.Sigmoid, bias=bcol(20 + ch))
                    gb = gbuf[ch % 2]
                    cp(DVE, gb[:, 0:30], halo[ch][:, 0:30])
                    stt(gb[:, 30:542], pa, bcol(16 + ch), sg, ALU.add, ALU.mult)
                    pcv = nps()
                    wc0 = pb + 44 + ch * 31
                    for k in range(31):
                        d = ndg()
                        ts(DVE, d, ident, pc[:, wc0 + k: wc0 + k + 1], ALU.mult)
                        mm(pcv, d, gb[:, k:k + 512], k == 0, k == 30)
                    act(acc[ch], pcv, AF.Identity, bias=pc[:, pb + 168 + ch: pb + 169 + ch])
                    cp(POOL, halo[ch][:, 0:30], gb[:, 512:542])
                    mm(pm, onesC, acc[ch], ch == 0, ch == 3)
                    sq = ntmp()
                    act(sq, acc[ch], AF.Square)
                    mm(pq, onesC, sq, ch == 0, ch == 3)
                    for _ in range(2):
                        if extra:
                            extra.pop(0)()

                pp = {0: PP(0)}
                for ch in range(4):
                    if ch + 1 < 4:
                        pp[ch + 1] = PP(ch + 1)
                    GLUCONV(ch, *pp[ch])
                pzs = {}
                for ch in range(2):
                    sz_ = next_slab()
                    pz = nps()
                    for c in range(8):
                        mm(pz, sz_[:, c * 128:(c + 1) * 128], xhi_t[c][qt], c == 0, c == 7)
                    pzs[ch] = pz
                ln_stats(pm, pq, mean_sb, rstd)
                for ch in range(4):
                    if ch in pzs:
                        pz = pzs[ch]
                    else:
                        sz_ = next_slab()
                        pz = nps()
                        for c in range(8):
                            mm(pz, sz_[:, c * 128:(c + 1) * 128], xhi_t[c][qt], c == 0, c == 7)
                    cn = ntmp()
                    tt(DVE, cn, acc[ch], mean_sb, ALU.subtract)
                    tt(DVE, cn, cn, rstd, ALU.mult)
                    gcol = pc[:, pb + 172 + ch: pb + 173 + ch]
                    bcl = pc[:, pb + 176 + ch: pb + 177 + ch]
                    lno = ntmp()
                    act(lno, cn, AF.Identity, bias=bcl, scale=gcol)
                    sg = ntmp()
                    act(sg, cn, AF.Sigmoid, bias=bcl, scale=gcol)
                    tt(DVE, lno, lno, sg, ALU.mult)
                    sgz = ntmp()
                    act(sgz, pz, AF.Sigmoid, bias=bcol(24 + ch))
                    zc = ntmp()
                    stt(zc, pz, bcol(24 + ch), sgz, ALU.add, ALU.mult)
                    tt(DVE, cg[ch], lno, zc, ALU.mult)
                while extra:
                    extra.pop(0)()

            def stage23(qt):
                nonlocal_acc = acc_box
                save = apos[0]
                apos[0] = nonlocal_acc[0][0].rng[0]
                merged = [carve(512, BF16, "mg%d" % i) for i in range(8)]
                apos[0] = save
                for j in range(8):
                    s_ap = next_slab()
                    pab = nps()
                    for c in range(4):
                        mm(pab, s_ap[:, c * 128:(c + 1) * 128], gatt_t[c][qt], c == 0, c == 3)
                    s_ga = next_slab()
                    pga = nps()
                    for c in range(8):
                        mm(pga, s_ga[:, c * 128:(c + 1) * 128], xhi_t[c][qt], c == 0, c == 7)
                    s_gc = next_slab()
                    pgc = nps()
                    for c in range(8):
                        mm(pgc, s_gc[:, c * 128:(c + 1) * 128], xhi_t[c][qt], c == 0, c == 7)
                    s_cp = next_slab()
                    pcb = nps()
                    for c in range(4):
                        mm(pcb, s_cp[:, c * 128:(c + 1) * 128], cg[c], c == 0, c == 3)
                    sga = ntmp()
                    act(sga, pga, AF.Sigmoid, bias=bcol(28 + j))
                    sgc = ntmp()
                    act(sgc, pgc, AF.Sigmoid, bias=bcol(36 + j))
                    m1 = ntmp()
                    tt(DVE, m1, pab, sga, ALU.mult)
                    m2_ = ntmp()
                    stt(m2_, pcb, pc[:, pb + 180 + j: pb + 181 + j], sgc, ALU.add, ALU.mult)
                    tt(DVE, merged[j], m1, m2_, ALU.add)
                save = apos[0]
                apos[0] = nonlocal_acc[0][0].rng[0]
                nonlocal_acc[0] = [carve(512, F32, "acc%d" % i) for i in range(4)]
                apos[0] = save
                for j in range(8):
                    s_o = next_slab()
                    py = nps()
                    for c in range(8):
                        mm(py, s_o[:, c * 128:(c + 1) * 128], merged[c], c == 0, c == 7)
                    xs = ntmp()
                    tt(DVE, xs, xhi_t[j][qt], xlo_t[j][qt], ALU.add)
                    r = ntmp()
                    stt(r, xs, ALPHA, py, ALU.mult, ALU.add)
                    sq = ntmp()
                    act(sq, r, AF.Square)
                    mm(pm, onesD, r, j == 0, j == 7)
                    mm(pq, onesD, sq, j == 0, j == 7)
                    act(xhi_t[j][qt], r, AF.Copy)
                    tt(DVE, xlo_t[j][qt], r, xhi_t[j][qt], ALU.subtract)
                ln_stats(pm, pq, mean2, rstd2)

            def norm_chunk(qt, j):
                r = ntmp()
                tt(DVE, r, xhi_t[j][qt], xlo_t[j][qt], ALU.add)
                tt(DVE, r, r, mean2, ALU.subtract)
                tt(DVE, r, r, rstd2, ALU.mult)
                xn = ntmp()
                act(xn, r, AF.Identity, bias=pc[:, pb + 196 + j: pb + 197 + j],
                    scale=pc[:, pb + 188 + j: pb + 189 + j])
                if l == L - 1:
                    dma_out(SP, outT[j * 128:(j + 1) * 128, qt * 512:(qt + 1) * 512], xn)
                    if xn not in out_store_tiles:
                        out_store_tiles.append(xn)
                else:
                    act(xhi_t[j][qt], xn, AF.Copy)
                    tt(DVE, xlo_t[j][qt], xn, xhi_t[j][qt], ALU.subtract)

            acc_box = [acc]
            pending = []
            for qt in range(NQ):
                acc = acc_box[0]
                stage1(qt, pending)
                stage23(qt)
                pending = [(lambda qt=qt, j=j: norm_chunk(qt, j)) for j in range(8)]
            while pending:
                pending.pop(0)()

        fin_waits = [(t.dsem, t.dsem.count) for t in out_store_tiles]
        SP.ops.append((fin_waits, None, None, 0))

        def replay(eng, e):
            for waits, fn, src, inc in eng.ops:
                for s, v in waits:
                    e.wait_ge(s.sem, v)
                if fn is not None:
                    fn(e).then_inc(src.sem, inc)

        with nc.Block() as block:
            @block.tensor
            def _(e):
                replay(PE, e)

            @block.scalar
            def _(e):
                replay(ACT, e)

            @block.vector
            def _(e):
                replay(DVE, e)

            @block.gpsimd
            def _(e):
                replay(POOL, e)

            @block.sync
            def _(e):
                replay(SP, e)
    return nc


def _consts():
    j = np.arange(128)[:, None]
    s = np.arange(128)[None, :]
    tri = (j >= s).astype(np.float32)
    msk = (j < s).astype(np.float32)
    return np.ascontiguousarray(np.concatenate([tri, msk, np.eye(128, dtype=np.float32)], axis=1))


def _prep_weights(inp, layers):
    L = len(layers)
    f = lambda a: np.ascontiguousarray(a, dtype=np.float32)
    w_in = np.asarray(inp["w_in"])[layers]
    wA = np.empty((L, 4, 128, 8, 512), np.float32)
    w4 = w_in.reshape(L, 8, 128, 5632)
    for hp in range(4):
        for j in range(4):
            blk = w4[:, :, :, j * 512 + hp * 128: j * 512 + hp * 128 + 128]
            wA[:, hp, :, :, j * 128:(j + 1) * 128] = blk.transpose(0, 2, 1, 3)
    wB = w4[:, :, :, 2048:].reshape(L, 8, 128, 28, 128).transpose(0, 3, 2, 1, 4)
    wAP = np.asarray(inp["w_att_proj"])[layers].reshape(L, 4, 128, 8, 128).transpose(0, 3, 2, 1, 4)
    wCP = np.asarray(inp["w_conv_proj"])[layers].reshape(L, 4, 128, 8, 128).transpose(0, 3, 2, 1, 4)
    wO = np.asarray(inp["w_out"])[layers].reshape(L, 8, 128, 8, 128).transpose(0, 3, 2, 1, 4)
    pcol = np.empty((128, L, NP), np.float32)
    b_in = np.asarray(inp["b_in"])[layers]
    pcol[:, :, 0:44] = b_in.reshape(L, 44, 128).transpose(2, 0, 1)
    cw = np.asarray(inp["conv_w"])[layers]
    pcol[:, :, 44:168] = cw.reshape(L, 31, 4, 128).transpose(3, 0, 2, 1).reshape(128, L, 124)
    def col(name, n):
        return np.asarray(inp[name])[layers].reshape(L, n, 128).transpose(2, 0, 1)
    pcol[:, :, 168:172] = col("conv_b", 4)
    pcol[:, :, 172:176] = col("conv_ln_g", 4)
    pcol[:, :, 176:180] = col("conv_ln_b", 4)
    pcol[:, :, 180:188] = col("b_conv_proj", 8)
    pcol[:, :, 188:196] = col("ln_g", 8)
    pcol[:, :, 196:204] = col("ln_b", 8)
    return {
        "wA": f(wA.reshape(L, 4, 128, 4096)),
        "wB": f(wB.reshape(L, 28, 128, 1024)),
        "wAP": f(wAP.reshape(L, 8, 128, 512)),
        "wCP": f(wCP.reshape(L, 8, 128, 512)),
        "wO": f(wO.reshape(L, 8, 128, 1024)),
        "pcol": f(pcol.reshape(128, L * NP)),
        "bv": f(b_in[:, 1024:1536]),
        "cst": _consts(),
    }


_NC_CACHE = {}


def _get_nc(S, L):
    if (S, L) not in _NC_CACHE:
        _NC_CACHE[(S, L)] = build(S, L)
    return _NC_CACHE[(S, L)]


def run_layers(xTs, inp, layers):
    S = xTs[0].shape[1]
    nc = _get_nc(S, len(layers))
    wts = _prep_weights(inp, layers)
    in_maps = [dict(wts, xT=np.ascontiguousarray(x)) for x in xTs]
    res = run_bass_kernel_spmd(nc, in_maps, core_ids=list(range(len(xTs))))
    return [r["outT"] for r in res.results]


def kernel(x, w_in, b_in, conv_w, conv_b, conv_ln_g, conv_ln_b,
           w_att_proj, w_conv_proj, b_conv_proj, w_out, ln_g, ln_b):
    inp = dict(w_in=w_in, b_in=b_in, conv_w=conv_w, conv_b=conv_b, conv_ln_g=conv_ln_g,
               conv_ln_b=conv_ln_b, w_att_proj=w_att_proj, w_conv_proj=w_conv_proj,
               b_conv_proj=b_conv_proj, w_out=w_out, ln_g=ln_g, ln_b=ln_b)
    x = np.asarray(x, dtype=np.float32)
    B = x.shape[0]
    xTs = [np.ascontiguousarray(x[b].T) for b in range(B)]
    if MODE == "fused":
        xTs = run_layers(xTs, inp, list(range(DEPTH)))
    else:
        for l in range(DEPTH):
            xTs = run_layers(xTs, inp, [l])
    return np.ascontiguousarray(np.stack([o.T for o in xTs], axis=0)).astype(np.float32)
```
